# Optimizing a Trainium2 kernel written in Bass

```python
import jax
import jax.numpy as jnp
from jax import lax
import numpy as np

D_MODEL = 4096
BATCH = 2
SEQ = 4096
DEPTH = 1
DEC_BATCH = 32
DEC_SEQ = 64
PAST_LEN = 4096

CHUNK = 64
MIX_WIDTH = D_MODEL
ATTN_WIDTH = MIX_WIDTH // 2
RET_WIDTH = MIX_WIDTH - ATTN_WIDTH
HEAD_DIM = 64
N_HEADS = ATTN_WIDTH // HEAD_DIM
N_KV_HEADS = 4
GROUP = N_HEADS // N_KV_HEADS
WINDOW = 128
WINDOW_CHUNKS = WINDOW // CHUNK
RET_HEADS = 8
RET_DV = RET_WIDTH // RET_HEADS
RET_DK = RET_DV // 2
ROPE_BASE = 10000.0
D_FF = 256 * ((8 * D_MODEL // 3 + 255) // 256)
PLE_DIM = 256
EPS = 1e-6

ATTN_Q_COLS = N_HEADS * HEAD_DIM
ATTN_KV_COLS = N_KV_HEADS * HEAD_DIM
RET_QK_COLS = RET_HEADS * RET_DK
SPLIT_POINTS = (
    ATTN_Q_COLS,
    ATTN_Q_COLS + ATTN_KV_COLS,
    ATTN_Q_COLS + 2 * ATTN_KV_COLS,
    ATTN_Q_COLS + 2 * ATTN_KV_COLS + RET_QK_COLS,
    ATTN_Q_COLS + 2 * ATTN_KV_COLS + 2 * RET_QK_COLS,
    ATTN_Q_COLS + 2 * ATTN_KV_COLS + 2 * RET_QK_COLS + RET_WIDTH,
)
IN_WIDTH = ATTN_Q_COLS + 2 * ATTN_KV_COLS + 2 * RET_QK_COLS + 2 * RET_WIDTH

kernel_name = "hymba_swa_sink_retention_macaron_ple_stream_step"


def rmsnorm(x, g):
    xf = x.astype(jnp.float32)
    y = xf * lax.rsqrt(jnp.mean(xf * xf, axis=-1, keepdims=True) + EPS)
    return (y * g.astype(jnp.float32)).astype(x.dtype)


def swiglu_half_step(h, g, w_gate, w_up, w_down):
    z = rmsnorm(h, g)
    return h + 0.5 * ((jax.nn.silu(z @ w_gate) * (z @ w_up)) @ w_down)


def ple_add(h, p, g, w_gate, w_proj):
    gate = jax.nn.sigmoid(rmsnorm(h, g) @ w_gate)
    return h + gate * (p @ w_proj)


def rotary(x, pos):
    half = x.shape[-1] // 2
    inv_freq = ROPE_BASE ** (-jnp.arange(half, dtype=jnp.float32) / half)
    ang = pos[:, None] * inv_freq[None, :]
    cos = jnp.cos(ang)[:, None, :]
    sin = jnp.sin(ang)[:, None, :]
    x1, x2 = x[..., :half], x[..., half:]
    return jnp.concatenate([x1 * cos - x2 * sin, x2 * cos + x1 * sin], axis=-1)


def retention_log_decay():
    return jnp.log(1.0 - 2.0 ** (-5.0 - jnp.arange(RET_HEADS, dtype=jnp.float32)))


def mixer_project(z, pos, w_in, g_q, g_k):
    B, L = z.shape[0], z.shape[1]
    qa, ka, va, qr, kr, vr, gr = jnp.split(z @ w_in, SPLIT_POINTS, axis=-1)
    qa = rmsnorm(qa.reshape(B, L, N_HEADS, HEAD_DIM), g_q)
    ka = rmsnorm(ka.reshape(B, L, N_KV_HEADS, HEAD_DIM), g_k)
    va = va.reshape(B, L, N_KV_HEADS, HEAD_DIM)
    qr = rotary(qr.reshape(B, L, RET_HEADS, RET_DK).astype(jnp.float32), pos)
    kr = rotary(kr.reshape(B, L, RET_HEADS, RET_DK).astype(jnp.float32), pos) * (RET_DK ** -0.5)
    vr = vr.reshape(B, L, RET_HEADS, RET_DV).astype(jnp.float32)
    return qa, ka, va, qr, kr, vr, gr


def sink_attend(q, k, v, sinks, mask):
    scores = jnp.einsum("...qkgd,...tkd->...kgqt", q, k).astype(jnp.float32) * (HEAD_DIM ** -0.5)
    if mask is not None:
        scores = jnp.where(mask, scores, -jnp.inf)
    sink = sinks.astype(jnp.float32).reshape(N_KV_HEADS, GROUP, 1, 1)
    m = jnp.maximum(scores.max(axis=-1, keepdims=True), sink)
    e = jnp.exp(scores - m)
    probs = e / (e.sum(axis=-1, keepdims=True) + jnp.exp(sink - m))
    return jnp.einsum("...kgqt,...tkd->...qkgd", probs.astype(v.dtype), v)


def attn_prompt(q, k, v, sinks):
    B, S = q.shape[0], q.shape[1]
    n_chunks = S // CHUNK
    qb = q.reshape(B, n_chunks, CHUNK, N_KV_HEADS, GROUP, HEAD_DIM)
    pad = ((0, 0), (WINDOW, 0), (0, 0), (0, 0))
    kp = jnp.pad(k, pad).reshape(B, n_chunks + WINDOW_CHUNKS, CHUNK, N_KV_HEADS, HEAD_DIM)
    vp = jnp.pad(v, pad).reshape(B, n_chunks + WINDOW_CHUNKS, CHUNK, N_KV_HEADS, HEAD_DIM)
    kb = jnp.concatenate([kp[:, j:j + n_chunks] for j in range(WINDOW_CHUNKS + 1)], axis=2)
    vb = jnp.concatenate([vp[:, j:j + n_chunks] for j in range(WINDOW_CHUNKS + 1)], axis=2)
    c = jnp.arange(n_chunks)[:, None]
    j = jnp.arange((WINDOW_CHUNKS + 1) * CHUNK)[None, :]
    valid = (c + j // CHUNK) >= WINDOW_CHUNKS
    mask = valid[:, None, None, None, :]
    o = sink_attend(qb, kb, vb, sinks, mask)
    return o.reshape(B, S, ATTN_WIDTH)


def retention_block(state, q, k, v, log_gamma):
    L = q.shape[1]
    idx = jnp.arange(L, dtype=jnp.float32)
    decay = jnp.exp(log_gamma[:, None, None] * jnp.abs(idx[:, None] - idx[None, :]))
    scores = jnp.einsum("blhd,bmhd->bhlm", q, k) * decay[None]
    out = jnp.einsum("bhlm,bmhv->blhv", scores, v)
    q_decay = jnp.exp(log_gamma[None, :] * (idx[:, None] + 1.0))
    out = out + jnp.einsum("blhd,bhdv->blhv", q, state) * q_decay[None, :, :, None]
    k_decay = jnp.exp(log_gamma[None, :] * (L - 1.0 - idx)[:, None])
    new_state = (jnp.exp(log_gamma * L)[None, :, None, None] * state
                 + jnp.einsum("blhd,blhv->bhdv", k * k_decay[None, :, :, None], v))
    return out, new_state


def retention_prompt(q, k, v, log_gamma):
    B, S = q.shape[0], q.shape[1]
    n_chunks = S // CHUNK

    def to_chunks(a):
        return jnp.swapaxes(a.reshape((B, n_chunks, CHUNK) + a.shape[2:]), 0, 1)

    state0 = jnp.zeros((B, RET_HEADS, RET_DK, RET_DV), jnp.float32)

    def step(state, xs):
        qc, kc, vc = xs
        out, state = retention_block(state, qc, kc, vc, log_gamma)
        return state, out

    state, out = lax.scan(step, state0, (to_chunks(q), to_chunks(k), to_chunks(v)))
    return jnp.swapaxes(out, 0, 1).reshape(B, S, RET_HEADS, RET_DV), state


def mixer_merge(attn_out, ret_out, gate, g_ret, w_out):
    B, L = gate.shape[0], gate.shape[1]
    mu = jnp.mean(ret_out, axis=-1, keepdims=True)
    var = jnp.mean(jnp.square(ret_out - mu), axis=-1, keepdims=True)
    normed = (ret_out - mu) * lax.rsqrt(var + EPS) * g_ret.astype(jnp.float32)
    ret = normed.reshape(B, L, RET_WIDTH).astype(gate.dtype) * jax.nn.silu(gate)
    return jnp.concatenate([attn_out, ret], axis=-1) @ w_out


def setup_inputs(seed: int = 0) -> dict:
    key = jax.random.key(seed)
    ks = jax.random.split(key, 32)
    f32 = jnp.float32

    def nrm(k, shape, scale=1.0):
        return jax.random.normal(k, shape, f32) * scale

    def gain(k, shape):
        return 1.0 + 0.05 * jax.random.normal(k, shape, f32)

    return {
        "x_prompt": nrm(ks[0], (BATCH, SEQ, D_MODEL)),
        "x_sample": nrm(ks[1], (DEC_BATCH, DEC_SEQ, D_MODEL)),
        "cache_attn_k": nrm(ks[2], (DEPTH, DEC_BATCH, WINDOW, N_KV_HEADS, HEAD_DIM)),
        "cache_attn_v": nrm(ks[3], (DEPTH, DEC_BATCH, WINDOW, N_KV_HEADS, HEAD_DIM)),
        "state_ret": nrm(ks[4], (DEPTH, DEC_BATCH, RET_HEADS, RET_DK, RET_DV)),
        "p_prompt": nrm(ks[5], (DEPTH, BATCH, SEQ, PLE_DIM)),
        "p_sample": nrm(ks[6], (DEPTH, DEC_BATCH, DEC_SEQ, PLE_DIM)),
        "g_ffn1": gain(ks[7], (DEPTH, D_MODEL)),
        "w_ffn1_gate": nrm(ks[8], (DEPTH, D_MODEL, D_FF), D_MODEL ** -0.5),
        "w_ffn1_up": nrm(ks[9], (DEPTH, D_MODEL, D_FF), D_MODEL ** -0.5),
        "w_ffn1_down": nrm(ks[10], (DEPTH, D_FF, D_MODEL), D_FF ** -0.5),
        "g_mix": gain(ks[11], (DEPTH, D_MODEL)),
        "w_in": nrm(ks[12], (DEPTH, D_MODEL, IN_WIDTH), D_MODEL ** -0.5),
        "g_q": gain(ks[13], (DEPTH, HEAD_DIM)),
        "g_k": gain(ks[14], (DEPTH, HEAD_DIM)),
        "attn_sinks": nrm(ks[15], (DEPTH, N_HEADS)),
        "g_ret": gain(ks[16], (DEPTH, RET_HEADS, RET_DV)),
        "w_out": nrm(ks[17], (DEPTH, MIX_WIDTH, D_MODEL), MIX_WIDTH ** -0.5),
        "g_ffn2": gain(ks[18], (DEPTH, D_MODEL)),
        "w_ffn2_gate": nrm(ks[19], (DEPTH, D_MODEL, D_FF), D_MODEL ** -0.5),
        "w_ffn2_up": nrm(ks[20], (DEPTH, D_MODEL, D_FF), D_MODEL ** -0.5),
        "w_ffn2_down": nrm(ks[21], (DEPTH, D_FF, D_MODEL), D_FF ** -0.5),
        "g_ple": gain(ks[22], (DEPTH, D_MODEL)),
        "w_ple_gate": nrm(ks[23], (DEPTH, D_MODEL, D_MODEL), D_MODEL ** -0.5),
        "w_ple_proj": nrm(ks[24], (DEPTH, PLE_DIM, D_MODEL), PLE_DIM ** -0.5),
    }


def reference(x_prompt, x_sample, cache_attn_k, cache_attn_v, state_ret, p_prompt, p_sample,
              g_ffn1, w_ffn1_gate, w_ffn1_up, w_ffn1_down, g_mix, w_in, g_q, g_k, attn_sinks,
              g_ret, w_out, g_ffn2, w_ffn2_gate, w_ffn2_up, w_ffn2_down, g_ple, w_ple_gate,
              w_ple_proj):
    log_gamma = retention_log_decay()
    dec_b, dec_len = x_sample.shape[0], x_sample.shape[1]
    pos_p = jnp.arange(x_prompt.shape[1], dtype=jnp.float32)
    pos_s = PAST_LEN + jnp.arange(dec_len, dtype=jnp.float32)

    hp, hs = x_prompt, x_sample
    kp_list, vp_list, sp_list, ks_list, vs_list, ss_list = [], [], [], [], [], []
    for i in range(DEPTH):
        hp = swiglu_half_step(hp, g_ffn1[i], w_ffn1_gate[i], w_ffn1_up[i], w_ffn1_down[i])
        hs = swiglu_half_step(hs, g_ffn1[i], w_ffn1_gate[i], w_ffn1_up[i], w_ffn1_down[i])

        qa, ka, va, qr, kr, vr, gr = mixer_project(rmsnorm(hp, g_mix[i]), pos_p, w_in[i], g_q[i], g_k[i])
        attn_o = attn_prompt(qa, ka, va, attn_sinks[i])
        ret_o, ret_state_p = retention_prompt(qr, kr, vr, log_gamma)
        hp = hp + mixer_merge(attn_o, ret_o, gr, g_ret[i], w_out[i])
        kp_list.append(ka[:, -WINDOW:])
        vp_list.append(va[:, -WINDOW:])
        sp_list.append(ret_state_p.astype(x_prompt.dtype))

        qa, ka, va, qr, kr, vr, gr = mixer_project(rmsnorm(hs, g_mix[i]), pos_s, w_in[i], g_q[i], g_k[i])
        k_all = jnp.concatenate([cache_attn_k[i], ka], axis=1)
        v_all = jnp.concatenate([cache_attn_v[i], va], axis=1)
        attn_o = sink_attend(qa.reshape(dec_b, dec_len, N_KV_HEADS, GROUP, HEAD_DIM), k_all, v_all,
                             attn_sinks[i], None).reshape(dec_b, dec_len, ATTN_WIDTH)
        ret_o, ret_state_s = retention_block(state_ret[i].astype(jnp.float32), qr, kr, vr, log_gamma)
        hs = hs + mixer_merge(attn_o, ret_o, gr, g_ret[i], w_out[i])
        ks_list.append(k_all[:, -WINDOW:])
        vs_list.append(v_all[:, -WINDOW:])
        ss_list.append(ret_state_s.astype(state_ret.dtype))

        hp = swiglu_half_step(hp, g_ffn2[i], w_ffn2_gate[i], w_ffn2_up[i], w_ffn2_down[i])
        hs = swiglu_half_step(hs, g_ffn2[i], w_ffn2_gate[i], w_ffn2_up[i], w_ffn2_down[i])

        hp = ple_add(hp, p_prompt[i], g_ple[i], w_ple_gate[i], w_ple_proj[i])
        hs = ple_add(hs, p_sample[i], g_ple[i], w_ple_gate[i], w_ple_proj[i])

    new_attn_k_prompt = jnp.stack(kp_list, axis=0)
    new_attn_v_prompt = jnp.stack(vp_list, axis=0)
    new_state_ret_prompt = jnp.stack(sp_list, axis=0)
    new_attn_k_sample = jnp.stack(ks_list, axis=0)
    new_attn_v_sample = jnp.stack(vs_list, axis=0)
    new_state_ret_sample = jnp.stack(ss_list, axis=0)
    return (hp, hs, new_attn_k_prompt, new_attn_v_prompt, new_state_ret_prompt,
            new_attn_k_sample, new_attn_v_sample, new_state_ret_sample)
```

```python
import numpy as np
from contextlib import ExitStack
import concourse.bass as bass
import concourse.mybir as mybir
from concourse.bass_utils import run_bass_kernel_spmd

F32 = mybir.dt.float32
BF16 = mybir.dt.bfloat16
AF = mybir.ActivationFunctionType
ALU = mybir.AluOpType
AX = mybir.AxisListType


class Tok:
    __slots__ = ("eng", "sem", "val", "key")

    def __init__(self, eng, sem, val, key):
        self.eng = eng
        self.sem = sem
        self.val = val
        self.key = key


class Buf:
    __slots__ = ("name", "w", "r", "war")

    def __init__(self, name=""):
        self.name = name
        self.w = {}
        self.r = {}
        self.war = {}


class Sched:
    def __init__(self, nc, es, n_dma_sems=10):
        self.nc = nc
        self.engs = {"pe": nc.tensor, "act": nc.scalar, "dve": nc.vector,
                     "pool": nc.gpsimd, "sp": nc.sync}
        self.csem = {}
        for k in ("pe", "act", "dve", "pool"):
            self.csem[k] = es.enter_context(nc.semaphore("c_" + k))
        self.ccnt = {k: 0 for k in self.csem}
        self.pending = {k: [] for k in self.csem}
        self.dsem = {}
        self.dcnt = {}
        self.drr = {}
        for q in ("sp", "pool"):
            self.dsem[q] = [es.enter_context(nc.semaphore("d_%s%d" % (q, i))) for i in range(n_dma_sems)]
            self.dcnt[q] = [0] * n_dma_sems
            self.drr[q] = 0
        self.waited = {k: {} for k in self.engs}
        self.n_ops = 0
        self.n_waits = 0

    def _wait(self, ek, tok):
        if tok.eng == "pe" and ek == "pe":
            return
        if tok.val is None:
            raise RuntimeError("unresolved token from %s needed by %s" % (tok.eng, ek))
        w = self.waited[ek]
        if w.get(tok.key, 0) >= tok.val:
            return
        self.engs[ek].wait_ge(tok.sem, tok.val)
        w[tok.key] = tok.val
        self.n_waits += 1

    def _deps(self, ek, reads, writes, appends):
        for b in reads:
            for t in b.w.values():
                self._wait(ek, t)
        for b in writes:
            for t in b.w.values():
                if t.eng != ek:
                    self._wait(ek, t)
            for t in b.r.values():
                if t.eng != ek:
                    self._wait(ek, t)
        for b in appends:
            for t in b.r.values():
                if t.eng != ek:
                    self._wait(ek, t)
            for t in b.war.values():
                if t.eng != ek:
                    self._wait(ek, t)

    def _post(self, tok, reads, writes, appends):
        for b in reads:
            b.r[tok.key] = tok
        for b in writes:
            b.w = {tok.key: tok}
            b.war = b.r
            b.r = {}
        for b in appends:
            b.w[tok.key] = tok

    def op(self, ek, fn, reads=(), writes=(), appends=(), signal=True):
        self._deps(ek, reads, writes, appends)
        ins = fn(self.engs[ek])
        tok = Tok(ek, self.csem[ek], None, "c_" + ek)
        if signal:
            self.ccnt[ek] += 1
            ins.then_inc(self.csem[ek], 1)
            tok.val = self.ccnt[ek]
            for p in self.pending[ek]:
                p.val = tok.val
            self.pending[ek] = []
        else:
            self.pending[ek].append(tok)
        self._post(tok, reads, writes, appends)
        self.n_ops += 1
        return tok

    def dma(self, q, out, in_, reads=(), writes=(), appends=()):
        self._deps(q, reads, writes, appends)
        i = self.drr[q]
        self.drr[q] = (i + 1) % len(self.dsem[q])
        sem = self.dsem[q][i]
        key = "d_%s%d" % (q, i)
        if self.dcnt[q][i] > 0:
            self._wait(q, Tok("dma", sem, self.dcnt[q][i], key))
        ins = self.engs[q].dma_start(out=out, in_=in_)
        self.dcnt[q][i] += 16
        ins.then_inc(sem, 16)
        tok = Tok("dma", sem, self.dcnt[q][i], key)
        self._post(tok, reads, writes, appends)
        self.n_ops += 1
        return tok

    def _all_tokens(self):
        toks = []
        for k in self.csem:
            if self.ccnt[k] > 0:
                toks.append(Tok("x" + k, self.csem[k], self.ccnt[k], "c_" + k))
        for q in self.dsem:
            for i, sem in enumerate(self.dsem[q]):
                if self.dcnt[q][i] > 0:
                    toks.append(Tok("dma", sem, self.dcnt[q][i], "d_%s%d" % (q, i)))
        return toks

    def barrier(self):
        toks = self._all_tokens()
        for ek in self.engs:
            for t in toks:
                if t.key == "c_pe" and ek == "pe":
                    continue
                self._wait(ek, t)

    def finish(self):
        for t in self._all_tokens():
            self._wait("sp", t)


D = 4096
KT = 32
HD = 64
NH = 32
NKV = 4
RH = 8
DK = 128
DV = 256
INW = 8704
NCT = INW // 128
PLE = 256
EPS = 1e-6
ROPE_BASE = 10000.0
NEG = -30000.0
SCALE = HD ** -0.5

G1, GM, G2, GP, GQ, GK, SINK, GRET, KDEC, COEF, SEL, NV = 0, 32, 64, 96, 128, 129, 130, 162, 178, 186, 250, 258
M_ID, M_BLK, M_PERM, M_DEC, M_QDEC, M_MA, M_MB, NM = 0, 128, 256, 384, 896, 1408, 1664, 1920


class Cfg:
    def __init__(self, SEG=1024, TP=512, NS=4, F=11008, AG=11, NW=5, ND=3, NCORES=8, CPS=4, XPAD=0, STOP=99, A4MAX=NCT, A4MIN=0, DBG=0):
        self.DBG = DBG
        self.A4MAX = A4MAX
        self.A4MIN = A4MIN
        self.XPAD = XPAD
        self.STOP = STOP
        self.SEG, self.TP, self.NS, self.F, self.AG, self.NW, self.ND = SEG, TP, NS, F, AG, NW, ND
        self.NCORES, self.CPS = NCORES, CPS
        self.NT = SEG + NS * 64
        self.NF = F // 128
        self.TM = max(TP, NS * 64)
        self.XR = 1024 + 128 + 128


def v3(ap, a, b):
    return ap.rearrange("p (a b) -> p a b", a=a, b=b)


def v4(ap, a, b, c):
    return ap.rearrange("p (a b c) -> p a b c", a=a, b=b, c=c)


class _Stop(Exception):
    pass


class Bump:
    def __init__(self, arena, start, end):
        self.arena, self.off, self.end = arena, start, end

    def take(self, n, dt=BF16):
        ne = n * (2 if dt == F32 else 1)
        ne = (ne + 15) // 16 * 16
        assert self.off + ne <= self.end, ("arena overflow", self.off, ne, self.end)
        ap = self.arena[:, self.off:self.off + ne]
        self.off += ne
        if dt == F32:
            ap = ap.bitcast(F32)
        return ap[:, :n]


def build(cfg):
    SEG, TP, NS, F, AG, NW, ND = cfg.SEG, cfg.TP, cfg.NS, cfg.F, cfg.AG, cfg.NW, cfg.ND
    NT, NF, TM, XR = cfg.NT, cfg.NF, cfg.TM, cfg.XR
    NC = cfg.NCORES
    nc = bass.Bass("TRN2", target_bir_lowering=False, num_devices=NC)

    def din(name, shape, dt=F32):
        return nc.dram_tensor(name, list(shape), dt, kind="ExternalInput").ap()

    def dout(name, shape, dt=F32):
        return nc.dram_tensor(name, list(shape), dt, kind="ExternalOutput").ap()

    def dscr(name, shape, dt=F32):
        return nc.dram_tensor(name, list(shape), dt, kind="Internal").ap()

    xT = din("xT", [D, NT]); pT = din("pT", [PLE, NT])
    ckT = din("ckT", [NS, 256, 128]); cvT = din("cvT", [NS, 256, 128]); cv = din("cv", [NS, 128, 256])
    st = din("st", [NS, RH, DK, DV])
    w1g = din("w1g", [D, F]); w1u = din("w1u", [D, F]); w1d = din("w1d", [F, D])
    w2g = din("w2g", [D, F]); w2u = din("w2u", [D, F]); w2d = din("w2d", [F, D])
    win = din("win", [D, INW]); wout = din("wout", [D, D]); wpg = din("wpg", [D, D]); wpp = din("wpp", [PLE, D])
    cvec_d = din("cvec", [128, NV]); cmat_d = din("cmat", [128, NM])
    ropeC_d = din("ropeC", [128, NT]); ropeS_d = din("ropeS", [128, NT])

    yT = dout("yT", [D, NT]); kaT_o = dout("kaT_o", [256, NT]); vaT_o = dout("vaT_o", [256, NT])
    ksT_o = dout("ksT_o", [NS, 256, 128]); vsT_o = dout("vsT_o", [NS, 256, 128])
    Lp_o = dout("Lp_o", [RH * DK, DV]); Ls_o = dout("Ls_o", [NS * RH * DK, DV])

    H1 = dscr("H1", [D, NT]); QA = dscr("QA", [2048, NT], BF16)
    KAX = dscr("KAX", [256, 128 + NT], BF16); VAX = dscr("VAX", [128 + NT, 256], BF16)
    QR = dscr("QR", [1024, NT], BF16); KR = dscr("KR", [1024, NT], BF16)
    KDt = dscr("KDt", [NT, 1024], BF16); VRt = dscr("VRt", [NT, 2048], BF16); GR = dscr("GR", [2048, NT], BF16)
    XS = dscr("XS", [XR, 256]); XD = dscr("XD", [NC * XR, 256])
    B = {n: Buf(n) for n in ("yT", "kaT_o", "vaT_o", "ksT_o", "vsT_o", "Lp_o", "Ls_o", "H1", "QA", "KAX", "VAX",
                             "QR", "KR", "KDt", "VRt", "GR", "XS", "XD")}

    gam = [1.0 - 2.0 ** (-5.0 - h) for h in range(RH)]
    g64 = [float(np.float32(np.exp(np.float32(np.log(np.float32(g))) * np.float32(64.0)))) for g in gam]

    es = ExitStack()
    with es:
        S = Sched(nc, es)

        def sb(n, s, d):
            return es.enter_context(nc.sbuf_tensor(n, s, d))

        hT = sb("hT", [128, KT, TM], F32); h_b = Buf("h")
        zT = sb("zT", [128, KT, TM], BF16); z_b = Buf("z")
        Lst = sb("Lst", [128, RH, DV], F32); L_b = Buf("L")
        cvec = sb("cvec_s", [128, NV], F32); cmat = sb("cmat_s", [128, NM], F32); c_b = Buf("c")
        identb = sb("identb", [128, 128], BF16); onesb = sb("onesb", [128, 128], BF16)
        ident32 = cmat[:, M_ID:M_ID + 128]; blk64 = cmat[:, M_BLK:M_BLK + 128]; perm = cmat[:, M_PERM:M_PERM + 128]
        decay = v3(cmat[:, M_DEC:M_DEC + 512], 8, 64); qdec = v3(cmat[:, M_QDEC:M_QDEC + 512], 8, 64)
        maskA = cmat[:, M_MA:M_MA + 256]; maskB = cmat[:, M_MB:M_MB + 256]
        ps = es.enter_context(nc.psum_tensor("ps", [128, 4096], F32))
        bk = [ps[:, i * 512:(i + 1) * 512] for i in range(8)]
        bk_b = [Buf("bk%d" % i) for i in range(8)]

        W_EL = NW * KT * 128
        N_EL = 2 * TM + 2 * TM
        X_FFN = 2 * AG * TM + ND * AG * 256 + 2 * 2 * TM
        X_A4 = 6 * 2 * TM + 2 * 2 * TM + 2 * TM + 2 * (TM // 128 + 1) * 128 + (TM // 128 + 1) * 8 * 128 + 2 * (TM // 128 + 1) * 256 + 256
        X_EL = max(X_FFN, X_A4) + 512 + cfg.XPAD
        ARENA = W_EL + N_EL + X_EL
        arena = sb("arena", [128, ARENA], BF16)
        perm_bytes = (KT * TM * 4 + KT * TM * 2 + RH * DV * 4 + NV * 4 + NM * 4 + 512 + ARENA * 2)
        assert perm_bytes <= 207 * 1024, ("SBUF budget", perm_bytes)
        wr = [v3(arena[:, i * KT * 128:(i + 1) * KT * 128], KT, 128) for i in range(NW)]
        wr_b = [Buf("wr%d" % i) for i in range(NW)]
        nb_ = Bump(arena, W_EL, W_EL + N_EL)
        rstd = nb_.take(TM, F32); rstd_b = Buf("rstd")
        sq = [nb_.take(TM), nb_.take(TM)]; sq_b = [Buf("sq0"), Buf("sq1")]
        X0, X1 = W_EL + N_EL, ARENA

        fb = Bump(arena, X0, X1)
        aT = [v3(fb.take(AG * TM), AG, TM) for _ in range(2)]; a_b = [Buf("a0"), Buf("a1")]
        dr = [v3(fb.take(AG * 256), AG, 256) for _ in range(ND)]; dr_b = [Buf("dr%d" % i) for i in range(ND)]
        stt_ = [fb.take(TM, F32) for _ in range(2)]; st_b = [Buf("st0"), Buf("st1")]

        wi = [0]
        di = [0]

        def load_w(w, c0, ncols=128, rows=KT):
            s = wi[0] % NW
            wi[0] += 1
            S.dma("pool", wr[s][:, :rows, :ncols], w[:, c0:c0 + ncols].rearrange("(kt p) f -> p kt f", p=128),
                  writes=[wr_b[s]])
            return s

        S.dma("sp", cvec[:], cvec_d, writes=[c_b])
        S.dma("sp", cmat[:], cmat_d, appends=[c_b])
        S.op("dve", lambda e: e.memset(onesb[:], 1.0), appends=[c_b])
        S.op("act", lambda e: e.activation(out=identb[:], in_=ident32, func=AF.Copy), reads=[c_b], appends=[c_b])
        S.op("dve", lambda e: e.memset(Lst[:], 0.0), writes=[L_b])

        def rmsnorm(T, goff):
            for kt in range(KT):
                i = kt % 2
                S.op("act", lambda e: e.activation(out=sq[i][:, :T], in_=hT[:, kt, :T], func=AF.Square),
                     reads=[h_b], writes=[sq_b[i]])
                S.op("pe", lambda e: e.matmul(bk[7][:, :T], onesb[:], sq[i][:, :T], start=(kt == 0), stop=(kt == KT - 1)),
                     reads=[sq_b[i], c_b], writes=[bk_b[7]])
            S.op("act", lambda e: e.activation(out=rstd[:, :T], in_=bk[7][:, :T], func=AF.Sqrt, bias=EPS, scale=1.0 / D),
                 reads=[bk_b[7]], writes=[rstd_b])
            S.op("dve", lambda e: e.reciprocal(out=rstd[:, :T], in_=rstd[:, :T]), reads=[rstd_b], writes=[rstd_b])
            for kt in range(KT):
                S.op("dve", lambda e: e.scalar_tensor_tensor(out=zT[:, kt, :T], in0=hT[:, kt, :T],
                                                             scalar=cvec[:, goff + kt:goff + kt + 1], in1=rstd[:, :T],
                                                             op0=ALU.mult, op1=ALU.mult),
                     reads=[h_b, c_b, rstd_b], writes=[z_b])

        def mm_group(bank_ap, bank_buf, slot, T, nk=KT, rhs3=None, rhs_b=None):
            rhs3 = zT if rhs3 is None else rhs3
            rhs_b = z_b if rhs_b is None else rhs_b

            def f(e):
                for kt in range(nk):
                    ins = e.matmul(bank_ap, wr[slot][:, kt, :], rhs3[:, kt, :T], start=(kt == 0), stop=(kt == nk - 1))
                return ins
            S.op("pe", f, reads=[wr_b[slot], rhs_b], writes=[bank_buf])

        def ffn(T, wg, wu, wd, goff):
            rmsnorm(T, goff)
            ngroups = (NF + AG - 1) // AG
            fcount = 0
            for g in range(ngroups):
                fts = list(range(g * AG, min(NF, (g + 1) * AG)))
                slot = g % 2
                for idx, j in enumerate(fts):
                    par = fcount % 2
                    fcount += 1
                    sg = load_w(wg, j * 128)
                    su = load_w(wu, j * 128)
                    mm_group(bk[par][:, :T], bk_b[par], sg, T)
                    mm_group(bk[2 + par][:, :T], bk_b[2 + par], su, T)
                    S.op("act", lambda e: e.activation(out=stt_[par][:, :T], in_=bk[par][:, :T], func=AF.Silu),
                         reads=[bk_b[par]], writes=[st_b[par]])
                    S.op("dve", lambda e: e.tensor_tensor(out=aT[slot][:, idx, :T], in0=stt_[par][:, :T],
                                                          in1=bk[2 + par][:, :T], op=ALU.mult),
                         reads=[st_b[par], bk_b[2 + par]], writes=[a_b[slot]])
                nf = len(fts)
                for dt2 in range(KT // 2):
                    s = di[0] % ND
                    di[0] += 1
                    S.dma("pool", dr[s][:, :nf, :],
                          wd[fts[0] * 128:(fts[-1] + 1) * 128, dt2 * 256:(dt2 + 1) * 256].rearrange("(j p) c -> p j c", p=128),
                          writes=[dr_b[s]])
                    for half in range(2):
                        dt = dt2 * 2 + half
                        bki = 4 + (dt % 2)

                        def mmd(e):
                            for idx in range(nf):
                                ins = e.matmul(bk[bki][:, :T], dr[s][:, idx, half * 128:(half + 1) * 128],
                                               aT[slot][:, idx, :T], start=(idx == 0), stop=(idx == nf - 1))
                            return ins
                        S.op("pe", mmd, reads=[dr_b[s], a_b[slot]], writes=[bk_b[bki]])
                        S.op("dve", lambda e: e.scalar_tensor_tensor(out=hT[:, dt, :T], in0=bk[bki][:, :T], scalar=0.5,
                                                                     in1=hT[:, dt, :T], op0=ALU.mult, op1=ALU.add),
                             reads=[bk_b[bki], h_b], writes=[h_b])

        def phase_a4(kind, col0, T):
            ab = Bump(arena, X0, X1)
            ftmp = [ab.take(TM, F32) for _ in range(6)]; ftmp_b = [Buf("ft%d" % i) for i in range(6)]
            ropeC = ab.take(TM, F32); ropeS = ab.take(TM, F32); rope_b = Buf("rope")
            obf = [ab.take(TM) for _ in range(2)]; obf_b = [Buf("ob0"), Buf("ob1")]
            nblk = max(1, T // 128)
            bs = min(T, 128)
            NBM = TM // 128 + 1
            tts = [v3(ab.take(NBM * 128), NBM, 128) for _ in range(2)]; tts_b = [Buf("tt0"), Buf("tt1")]
            kd_sb = v4(ab.take(NBM * 8 * 128), NBM, 8, 128); kd_b = Buf("kd")
            vrt = [v3(ab.take(NBM * 256), NBM, 256) for _ in range(2)]; vrt_b = [Buf("vrt0"), Buf("vrt1")]
            fi = [0]
            oi = [0]
            ti = [0]

            def ft():
                i = fi[0] % 6
                fi[0] += 1
                return ftmp[i], ftmp_b[i]

            def ob():
                i = oi[0] % 2
                oi[0] += 1
                return obf[i], obf_b[i]

            def transposes(src, src_b, which):
                bi = 4 + which % 2
                tbv = bk[bi].bitcast(BF16)

                def f(e):
                    for tb in range(nblk):
                        ins = e.transpose(out=tbv[:bs, tb * 128:(tb + 1) * 128], in_=src[:, tb * 128:tb * 128 + bs],
                                          identity=identb[:])
                    return ins
                S.op("pe", f, reads=[src_b, c_b], writes=[bk_b[bi]])
                return v3(tbv[:bs, :nblk * 128], nblk, 128), bk_b[bi]

            def tok_dst(dram, tok0, c0):
                return dram[tok0:tok0 + T, c0:c0 + 128].rearrange("(b p) c -> p b c", p=bs)

            S.dma("sp", ropeC[:, :T], ropeC_d[:, col0:col0 + T], writes=[rope_b])
            S.dma("sp", ropeS[:, :T], ropeS_d[:, col0:col0 + T], appends=[rope_b])
            cols = slice(col0, col0 + T)
            for ct in range(cfg.A4MIN, cfg.A4MAX):
                slot = load_w(win, ct * 128)
                mb = ct % 2
                bank, bank_b = bk[mb][:, :T], bk_b[mb]
                aux, aux_b = bk[2 + mb][:, :T], bk_b[2 + mb]
                mm_group(bank, bank_b, slot, T)
                rows = None
                if ct < 18:
                    f, f_b = ft()
                    S.op("act", lambda e: e.activation(out=f[:, :T], in_=bank, func=AF.Square), reads=[bank_b], writes=[f_b])
                    S.op("pe", lambda e: e.matmul(aux, blk64, f[:, :T], start=True, stop=True), reads=[f_b, c_b], writes=[aux_b])
                    r, r_b = ft()
                    S.op("act", lambda e: e.activation(out=r[:, :T], in_=aux, func=AF.Sqrt, bias=EPS, scale=1.0),
                         reads=[aux_b], writes=[r_b])
                    S.op("dve", lambda e: e.reciprocal(out=r[:, :T], in_=r[:, :T]), reads=[r_b], writes=[r_b])
                    if ct < 16:
                        o, o_b = ob()
                        S.op("dve", lambda e: e.scalar_tensor_tensor(out=o[:, :T], in0=bank, scalar=cvec[:, GQ:GQ + 1],
                                                                     in1=r[:, :T], op0=ALU.mult, op1=ALU.mult),
                             reads=[bank_b, r_b, c_b], writes=[o_b])
                        S.dma("sp", QA[ct * 128:(ct + 1) * 128, cols], o[:, :T], reads=[o_b], appends=[B["QA"]])
                    else:
                        o, o_b = ft()
                        rws = slice((ct - 16) * 128, (ct - 15) * 128)
                        S.op("dve", lambda e: e.scalar_tensor_tensor(out=o[:, :T], in0=bank, scalar=cvec[:, GK:GK + 1],
                                                                     in1=r[:, :T], op0=ALU.mult, op1=ALU.mult),
                             reads=[bank_b, r_b, c_b], writes=[o_b])
                        S.dma("sp", kaT_o[rws, cols], o[:, :T], reads=[o_b], appends=[B["kaT_o"]])
                        S.dma("pool", KAX[rws, 128 + col0:128 + col0 + T], o[:, :T], reads=[o_b], appends=[B["KAX"]])
                        if kind == "s":
                            for s_ in range(NS):
                                S.dma("sp", ksT_o[s_, rws, 64:128], o[:, s_ * 64:(s_ + 1) * 64], reads=[o_b],
                                      appends=[B["ksT_o"]])
                elif ct < 20:
                    rws = slice((ct - 18) * 128, (ct - 17) * 128)
                    o, o_b = ft()
                    S.op("act", lambda e: e.activation(out=o[:, :T], in_=bank, func=AF.Copy), reads=[bank_b], writes=[o_b])
                    S.dma("sp", vaT_o[rws, cols], o[:, :T], reads=[o_b], appends=[B["vaT_o"]])
                    if kind == "s":
                        for s_ in range(NS):
                            S.dma("sp", vsT_o[s_, rws, 64:128], o[:, s_ * 64:(s_ + 1) * 64], reads=[o_b],
                                  appends=[B["vsT_o"]])
                    o2, o2_b = ob()
                    S.op("dve", lambda e: e.tensor_copy(out=o2[:, :T], in_=o[:, :T]), reads=[o_b], writes=[o2_b])
                    tv, tv_b = transposes(o2, o2_b, ct)
                    k = ti[0] % 2
                    ti[0] += 1
                    S.op("act", lambda e: e.activation(out=tts[k][:bs, :nblk, :], in_=tv, func=AF.Copy),
                         reads=[tv_b], writes=[tts_b[k]])
                    S.dma("sp", tok_dst(VAX, 128 + col0, (ct - 18) * 128), tts[k][:bs, :nblk, :], reads=[tts_b[k]],
                          appends=[B["VAX"]])
                elif ct < 36:
                    isq = ct < 28
                    h = ct - (20 if isq else 28)
                    u, u_b = ft()
                    S.op("act", lambda e: e.activation(out=u[:, :T], in_=bank, func=AF.Copy), reads=[bank_b], writes=[u_b])
                    S.op("pe", lambda e: e.matmul(aux, perm, u[:, :T], start=True, stop=True), reads=[u_b, c_b], writes=[aux_b])
                    t1, t1_b = ft()
                    S.op("dve", lambda e: e.tensor_tensor(out=t1[:, :T], in0=u[:, :T], in1=ropeC[:, :T], op=ALU.mult),
                         reads=[u_b, rope_b], writes=[t1_b])
                    t2, t2_b = ft()
                    S.op("dve", lambda e: e.tensor_tensor(out=t2[:, :T], in0=aux, in1=ropeS[:, :T], op=ALU.mult),
                         reads=[aux_b, rope_b], writes=[t2_b])
                    o, o_b = ob()
                    S.op("dve", lambda e: e.tensor_tensor(out=o[:, :T], in0=t1[:, :T], in1=t2[:, :T], op=ALU.add),
                         reads=[t1_b, t2_b], writes=[o_b])
                    if isq:
                        S.dma("sp", QR[h * 128:(h + 1) * 128, cols], o[:, :T], reads=[o_b], appends=[B["QR"]])
                    else:
                        S.dma("sp", KR[h * 128:(h + 1) * 128, cols], o[:, :T], reads=[o_b], appends=[B["KR"]])
                        tv, tv_b = transposes(o, o_b, ct)
                        S.op("dve", lambda e: e.tensor_scalar(out=kd_sb[:bs, :nblk, h, :], in0=tv,
                                                              scalar1=cvec[:bs, KDEC + h:KDEC + h + 1], scalar2=None,
                                                              op0=ALU.mult),
                             reads=[tv_b, c_b], appends=[kd_b])
                        S.dma("sp", tok_dst(KDt, col0, h * 128), kd_sb[:bs, :nblk, h, :], reads=[kd_b], appends=[B["KDt"]])
                elif ct < 52:
                    h, j = (ct - 36) // 2, (ct - 36) % 2
                    o, o_b = ob()
                    S.op("act", lambda e: e.activation(out=o[:, :T], in_=bank, func=AF.Copy), reads=[bank_b], writes=[o_b])
                    tv, tv_b = transposes(o, o_b, ct)
                    hs = h % 2
                    S.op("dve", lambda e: e.tensor_copy(out=vrt[hs][:bs, :nblk, j * 128:(j + 1) * 128], in_=tv),
                         reads=[tv_b], writes=([vrt_b[hs]] if j == 0 else []), appends=([vrt_b[hs]] if j == 1 else []))
                    S.dma("sp", tok_dst(VRt, col0, h * 256 + j * 128), vrt[hs][:bs, :nblk, j * 128:(j + 1) * 128],
                          reads=[vrt_b[hs]], appends=[B["VRt"]])
                    if j == 1 and kind == "p":
                        for c in range(T // 64):
                            tb, hf = c // 2, c % 2
                            rws = slice(hf * 64, (hf + 1) * 64)
                            bi = 6
                            pb = bk[bi][:, (c % 2) * 256:(c % 2 + 1) * 256]
                            S.op("pe", lambda e: e.matmul(pb, kd_sb[rws, tb, h, :], vrt[hs][rws, tb, :], start=True, stop=True),
                                 reads=[kd_b, vrt_b[hs]], writes=[bk_b[bi]])
                            S.op("dve", lambda e: e.scalar_tensor_tensor(out=Lst[:, h, :], in0=Lst[:, h, :], scalar=g64[h],
                                                                         in1=pb, op0=ALU.mult, op1=ALU.add),
                                 reads=[bk_b[bi], L_b], writes=[L_b])
                else:
                    i = ct - 52
                    f, f_b = ft()
                    S.op("act", lambda e: e.activation(out=f[:, :T], in_=bank, func=AF.Silu), reads=[bank_b], writes=[f_b])
                    o, o_b = ob()
                    S.op("dve", lambda e: e.tensor_scalar(out=o[:, :T], in0=f[:, :T], scalar1=cvec[:, GRET + i:GRET + i + 1],
                                                          scalar2=None, op0=ALU.mult), reads=[f_b, c_b], writes=[o_b])
                    S.dma("sp", GR[i * 128:(i + 1) * 128, cols], o[:, :T], reads=[o_b], appends=[B["GR"]])

        def mixer(kind, col0, T):
            mbs = [Bump(arena, X0, X1), Bump(arena, 0, W_EL), Bump(arena, W_EL, W_EL + N_EL)]

            def take(n, dt=BF16):
                ne = (n * (2 if dt == F32 else 1) + 15) // 16 * 16
                for mb_ in mbs:
                    if mb_.off + ne <= mb_.end:
                        return mb_.take(n, dt)
                raise AssertionError("mixer arena overflow")

            NBM = TM // 128 + 1
            kk = [take(128 + TM) for _ in range(2)]; kk_b = [Buf("kk0"), Buf("kk1")]
            vv = [v3(take(NBM * 128), NBM, 128) for _ in range(2)]; vv_b = [Buf("vv0"), Buf("vv1")]
            qq0 = v3(take(4 * TM), 4, TM); qq = [qq0, qq0]; qq0_b = Buf("qq0"); qq_b = [qq0_b, qq0_b]
            s_sb = v3(take(8 * 256, F32), 8, 256); s_b = Buf("s")
            Pm = v3(take(8 * 256), 8, 256); Pm_b = Buf("Pm")
            PTf = take(2 * 8 * 128); PT_b = Buf("PT")
            sm = [take(8, F32) for _ in range(7)]; sm_b = [Buf("sm%d" % i) for i in range(7)]
            sc_all = ps[:, 0:2048]
            ptb_all = ps[:, 2048:3072].bitcast(BF16)

            def attn_unit(u, nq, kbl, kk_t, kk_tb, koff, vv_t, vv_tb, vblk0, qq_t, qq_tb, qoff, mask, g, dcol):
                if cfg.DBG in (10, 14):
                    return
                nk = sum(kbl)
                sc3 = v3(sc_all, 8, 256)[:nq, :, :nk]

                def fsc(e):
                    for hh in range(8):
                        j, hf = hh // 2, hh % 2
                        sl = hf * 4 + j
                        ins = e.matmul(sc_all[:nq, sl * 256:sl * 256 + nk], qq_t[hf * 64:(hf + 1) * 64, j, qoff:qoff + nq],
                                       kk_t[hf * 64:(hf + 1) * 64, koff:koff + nk], start=True, stop=True)
                    return ins
                S.op("pe", fsc, reads=[qq_tb, kk_tb], writes=bk_b[0:4])
                sv = s_sb[:nq, :, :nk]
                if mask is not None:
                    S.op("dve", lambda e: e.tensor_tensor(out=sv, in0=sc3, in1=mask[:nq, :nk].unsqueeze(1).to_broadcast([nq, 8, nk]),
                                                          op=ALU.add), reads=bk_b[0:4] + [c_b], writes=[s_b])
                    src, src_bs = sv, [s_b]
                else:
                    src, src_bs = sc3, bk_b[0:4]
                mx, m_, rs, dd, es_, den, rinv = [x[:nq, :] for x in sm]
                S.op("dve", lambda e: e.tensor_reduce(out=mx, in_=src, axis=AX.X, op=ALU.max), reads=src_bs, writes=[sm_b[0]])
                sk = cvec[:nq, SINK + g * 8:SINK + g * 8 + 8]
                S.op("dve", lambda e: e.scalar_tensor_tensor(out=m_, in0=mx, scalar=SCALE, in1=sk, op0=ALU.mult, op1=ALU.max),
                     reads=[sm_b[0], c_b], writes=[sm_b[1]])
                S.op("dve", lambda e: e.scalar_tensor_tensor(out=sv, in0=src, scalar=SCALE,
                                                             in1=m_.unsqueeze(2).to_broadcast([nq, 8, nk]),
                                                             op0=ALU.mult, op1=ALU.subtract),
                     reads=src_bs + [sm_b[1]], writes=[s_b])
                S.op("act", lambda e: e.activation(out=sv, in_=sv, func=AF.Exp), reads=[s_b], writes=[s_b])
                S.op("dve", lambda e: e.tensor_reduce(out=rs, in_=sv, axis=AX.X, op=ALU.add), reads=[s_b], writes=[sm_b[2]])
                S.op("dve", lambda e: e.tensor_tensor(out=dd, in0=sk, in1=m_, op=ALU.subtract), reads=[sm_b[1], c_b],
                     writes=[sm_b[3]])
                S.op("act", lambda e: e.activation(out=es_, in_=dd, func=AF.Exp), reads=[sm_b[3]], writes=[sm_b[4]])
                S.op("dve", lambda e: e.tensor_tensor(out=den, in0=rs, in1=es_, op=ALU.add), reads=[sm_b[2], sm_b[4]],
                     writes=[sm_b[5]])
                S.op("dve", lambda e: e.reciprocal(out=rinv, in_=den), reads=[sm_b[5]], writes=[sm_b[6]])
                if cfg.DBG == 11:
                    return
                pv = Pm[:nq, :, :nk]
                S.op("dve", lambda e: e.tensor_tensor(out=pv, in0=sv, in1=rinv.unsqueeze(2).to_broadcast([nq, 8, nk]),
                                                      op=ALU.mult), reads=[s_b, sm_b[6]], writes=[Pm_b])
                ptb4 = ptb_all.rearrange("p (k x) -> p k x", k=2)[:, :, :8 * nq].rearrange("p k (h q) -> p k h q", q=nq)
                PT4 = v4(PTf[:, :2 * 8 * nq], 2, 8, nq)

                def ftr(e):
                    k0 = 0
                    for kb, kn in enumerate(kbl):
                        for hh in range(8):
                            ins = e.transpose(out=ptb4[:kn, kb, hh, :], in_=Pm[:nq, hh, k0:k0 + kn], identity=identb[:nq, :nq])
                        k0 += kn
                    return ins
                S.op("pe", ftr, reads=[Pm_b, c_b], writes=bk_b[4:6])
                S.op("act", lambda e: e.activation(out=PT4[:kbl[0], 0, :, :], in_=ptb4[:kbl[0], 0, :, :], func=AF.Copy),
                     reads=bk_b[4:6], writes=[PT_b])
                S.op("dve", lambda e: e.tensor_copy(out=PT4[:kbl[1], 1, :, :], in_=ptb4[:kbl[1], 1, :, :]),
                     reads=bk_b[4:6], appends=[PT_b])
                if cfg.DBG == 12:
                    return
                nb = 512 // nq
                for bq in range(8 // nb):
                    obk = bk[6 + bq % 2]

                    def fpv(e):
                        for kb, kn in enumerate(kbl):
                            ins = e.matmul(obk[:, :nb * nq], vv_t[:kn, vblk0 + kb, :], PT4[:kn, kb, bq * nb:(bq + 1) * nb, :],
                                           start=(kb == 0), stop=(kb == len(kbl) - 1))
                        return ins
                    S.op("pe", fpv, reads=[vv_tb, PT_b], writes=[bk_b[6 + bq % 2]])
                    for hf in range(2):
                        if (hf * 4) // nb != bq:
                            continue
                        i0 = (hf * 4) % nb
                        S.op("act", lambda e: e.activation(
                            out=zT[hf * 64:(hf + 1) * 64, 4 * g:4 * g + 4, dcol:dcol + nq],
                            in_=v3(obk[hf * 64:(hf + 1) * 64, i0 * nq:(i0 + 4) * nq], 4, nq), func=AF.Copy),
                            reads=[bk_b[6 + bq % 2]], appends=[z_b])

            un = 0
            if kind == "p":
                npairs = T // 128
                nblk = T // 128 + 1
                for g in range(NKV):
                    k = g % 2
                    for hf in range(2):
                        S.dma("sp", kk[k][hf * 64:(hf + 1) * 64, :128 + T], KAX[g * 64:(g + 1) * 64, col0:col0 + 128 + T],
                              reads=[B["KAX"]], writes=([kk_b[k]] if hf == 0 else []), appends=([kk_b[k]] if hf else []))
                        S.dma("sp", vv[k][:, :nblk, hf * 64:(hf + 1) * 64],
                              VAX[col0:col0 + 128 + T, g * 64:(g + 1) * 64].rearrange("(b p) c -> p b c", p=128),
                              reads=[B["VAX"]], writes=([vv_b[k]] if hf == 0 else []), appends=([vv_b[k]] if hf else []))
                    S.dma("sp", qq[k][:, :, :T], QA[g * 512:(g + 1) * 512, col0:col0 + T].rearrange("(j p) t -> p j t", p=128),
                          reads=[B["QA"]], writes=[qq_b[k]])
                    for i in range(npairs):
                        mask = maskA if (col0 == 0 and i == 0) else maskB
                        attn_unit(un, 128, [128, 128], kk[k], kk_b[k], i * 128, vv[k], vv_b[k], i, qq[k], qq_b[k], i * 128,
                                  mask, g, i * 128)
                        un += 1
            else:
                for s_ in range(NS):
                    for g in range(NKV):
                        k = un % 2
                        tok0 = SEG + s_ * 64
                        for hf in range(2):
                            rws = slice(hf * 64, (hf + 1) * 64)
                            S.dma("pool", kk[k][rws, 0:128], ckT[s_, g * 64:(g + 1) * 64, :],
                                  writes=([kk_b[k]] if hf == 0 else []), appends=([kk_b[k]] if hf else []))
                            S.dma("sp", kk[k][rws, 128:192], KAX[g * 64:(g + 1) * 64, 128 + tok0:128 + tok0 + 64],
                                  reads=[B["KAX"]], appends=[kk_b[k]])
                            S.dma("pool", vv[k][:, 0, rws], cv[s_, :, g * 64:(g + 1) * 64],
                                  writes=([vv_b[k]] if hf == 0 else []), appends=([vv_b[k]] if hf else []))
                            S.dma("sp", vv[k][0:64, 1, rws], VAX[128 + tok0:128 + tok0 + 64, g * 64:(g + 1) * 64],
                                  reads=[B["VAX"]], appends=[vv_b[k]])
                        S.dma("sp", qq[k][:, :, :64], QA[g * 512:(g + 1) * 512, tok0:tok0 + 64].rearrange("(j p) t -> p j t", p=128),
                              reads=[B["QA"]], writes=[qq_b[k]])
                        attn_unit(un, 64, [128, 64], kk[k], kk_b[k], 0, vv[k], vv_b[k], 0, qq[k], qq_b[k], 0, None, g, s_ * 64)
                        un += 1

            qrT = v3(take(8 * TM), 8, TM); krT = v3(take(8 * TM), 8, TM); GRt = v3(take(16 * TM), 16, TM)
            qk_b = Buf("qk"); gr_b = Buf("gr")
            kdc = [take(1024) for _ in range(2)]; vrc = [take(2048) for _ in range(2)]
            kv_b = [Buf("kv0"), Buf("kv1")]
            ATt = v3(take(4 * 64), 4, 64); AT_b = Buf("AT")
            qd = v3(take(4 * 64), 4, 64); qd_b = Buf("qd")
            sqf = v3(take(4 * 256, F32), 4, 256); sqf_b = Buf("sqf")
            xn = v3(take(4 * 256, F32), 4, 256); xn_b = Buf("xn")
            xnb = v3(take(4 * 256), 4, 256); xnb_b = Buf("xnb")
            Sbf = v3(take(8 * 256), 8, 256); Sbf_b = Buf("Sbf")
            rsm = [take(4, F32) for _ in range(6)]; rsm_b = [Buf("rsm%d" % i) for i in range(6)]
            S.dma("sp", qrT[:, :, :T], QR[:, col0:col0 + T].rearrange("(h p) t -> p h t", p=128), reads=[B["QR"]], writes=[qk_b])
            S.dma("sp", krT[:, :, :T], KR[:, col0:col0 + T].rearrange("(h p) t -> p h t", p=128), reads=[B["KR"]], appends=[qk_b])
            S.dma("sp", GRt[:, :, :T], GR[:, col0:col0 + T].rearrange("(i p) t -> p i t", p=128), reads=[B["GR"]], writes=[gr_b])

            def ret_chunk(ci, tcol, tok0):
                if cfg.DBG in (10, 11, 12, 13):
                    return
                k = ci % 2
                S.dma("sp", kdc[k][0:64, :], KDt[tok0:tok0 + 64, :], reads=[B["KDt"]], writes=[kv_b[k]])
                S.dma("sp", vrc[k][0:64, :], VRt[tok0:tok0 + 64, :], reads=[B["VRt"]], appends=[kv_b[k]])
                for bq in range(2):
                    h0 = bq * 4
                    scb = bk[0]

                    def fs(e):
                        for hh in range(4):
                            ins = e.matmul(scb[0:64, hh * 64:(hh + 1) * 64], krT[:, h0 + hh, tcol:tcol + 64],
                                           qrT[:, h0 + hh, tcol:tcol + 64], start=True, stop=True)
                        return ins
                    S.op("pe", fs, reads=[qk_b], writes=[bk_b[0]])
                    S.op("dve", lambda e: e.tensor_tensor(out=ATt[0:64, :, :], in0=v3(scb[0:64, 0:256], 4, 64),
                                                          in1=decay[0:64, h0:h0 + 4, :], op=ALU.mult),
                         reads=[bk_b[0], c_b], writes=[AT_b])
                    S.op("dve", lambda e: e.tensor_tensor(out=qd[:, :, :], in0=qrT[:, h0:h0 + 4, tcol:tcol + 64],
                                                          in1=qdec[:, h0:h0 + 4, :], op=ALU.mult),
                         reads=[qk_b, c_b], writes=[qd_b])
                    ob2 = ps[:, 512:1536]

                    def fo(e):
                        for hh in range(4):
                            h = h0 + hh
                            e.matmul(ob2[0:64, hh * 256:(hh + 1) * 256], ATt[0:64, hh, :], vrc[k][0:64, h * 256:(h + 1) * 256],
                                     start=True, stop=False)
                            ins = e.matmul(ob2[0:64, hh * 256:(hh + 1) * 256], qd[:, hh, :], Sbf[:, h, :], start=False, stop=True)
                        return ins
                    S.op("pe", fo, reads=[AT_b, qd_b, kv_b[k], Sbf_b], writes=bk_b[1:3])
                    o3 = v3(ob2[0:64, :], 4, 256)
                    sm_, ssq, mean, msq, var, rstd_ = [x[0:64, :] for x in rsm]
                    S.op("dve", lambda e: e.tensor_reduce(out=sm_, in_=o3, axis=AX.X, op=ALU.add), reads=bk_b[1:3], writes=[rsm_b[0]])
                    S.op("act", lambda e: e.activation(out=sqf[0:64, :, :], in_=o3, func=AF.Square), reads=bk_b[1:3] + [rsm_b[0]],
                         writes=[sqf_b])
                    S.op("dve", lambda e: e.tensor_reduce(out=ssq, in_=sqf[0:64, :, :], axis=AX.X, op=ALU.add), reads=[sqf_b],
                         writes=[rsm_b[1]])
                    S.op("dve", lambda e: e.tensor_scalar(out=mean, in0=sm_, scalar1=1.0 / DV, scalar2=None, op0=ALU.mult),
                         reads=[rsm_b[0]], writes=[rsm_b[2]])
                    S.op("dve", lambda e: e.tensor_tensor(out=msq, in0=mean, in1=mean, op=ALU.mult), reads=[rsm_b[2]],
                         writes=[rsm_b[3]])
                    S.op("dve", lambda e: e.scalar_tensor_tensor(out=var, in0=ssq, scalar=1.0 / DV, in1=msq, op0=ALU.mult,
                                                                 op1=ALU.subtract), reads=[rsm_b[1], rsm_b[3]], writes=[rsm_b[4]])
                    S.op("act", lambda e: e.activation(out=rstd_, in_=var, func=AF.Sqrt, bias=EPS, scale=1.0), reads=[rsm_b[4]],
                         writes=[rsm_b[5]])
                    S.op("dve", lambda e: e.reciprocal(out=rstd_, in_=rstd_), reads=[rsm_b[5]], writes=[rsm_b[5]])
                    S.op("dve", lambda e: e.tensor_tensor(out=xn[0:64, :, :], in0=o3,
                                                          in1=mean.unsqueeze(2).to_broadcast([64, 4, 256]), op=ALU.subtract),
                         reads=bk_b[1:3] + [rsm_b[2]], writes=[xn_b])
                    S.op("dve", lambda e: e.tensor_tensor(out=xnb[0:64, :, :], in0=xn[0:64, :, :],
                                                          in1=rstd_.unsqueeze(2).to_broadcast([64, 4, 256]), op=ALU.mult),
                         reads=[xn_b, rsm_b[5]], writes=[xnb_b])
                    tpb = bk[3].bitcast(BF16)

                    def ftp(e):
                        for hh in range(4):
                            for j in range(2):
                                ins = e.transpose(out=tpb[:, (hh * 2 + j) * 64:(hh * 2 + j + 1) * 64],
                                                  in_=xnb[0:64, hh, j * 128:(j + 1) * 128], identity=identb[0:64, 0:64])
                        return ins
                    S.op("pe", ftp, reads=[xnb_b, c_b], writes=[bk_b[3]])
                    S.op("dve", lambda e: e.tensor_tensor(out=zT[:, 16 + 2 * h0:16 + 2 * h0 + 8, tcol:tcol + 64],
                                                          in0=v3(tpb[:, 0:512], 8, 64),
                                                          in1=GRt[:, 2 * h0:2 * h0 + 8, tcol:tcol + 64], op=ALU.mult),
                         reads=[bk_b[3], gr_b], appends=[z_b])
                    stb = ps[:, 2048 + 512 * 0:2048 + 1024] if False else ps[:, 2048:3072]

                    def fst(e):
                        for hh in range(4):
                            h = h0 + hh
                            ins = e.matmul(stb[:, hh * 256:(hh + 1) * 256], kdc[k][0:64, h * 128:(h + 1) * 128],
                                           vrc[k][0:64, h * 256:(h + 1) * 256], start=True, stop=True)
                        return ins
                    S.op("pe", fst, reads=[kv_b[k]], writes=bk_b[4:6])
                    for hh in range(4):
                        h = h0 + hh
                        S.op("dve", lambda e: e.scalar_tensor_tensor(out=Lst[:, h, :], in0=Lst[:, h, :], scalar=g64[h],
                                                                     in1=stb[:, hh * 256:(hh + 1) * 256], op0=ALU.mult, op1=ALU.add),
                             reads=bk_b[4:6] + [L_b], writes=[L_b])
                    S.op("act", lambda e: e.activation(out=Sbf[:, h0:h0 + 4, :], in_=Lst[:, h0:h0 + 4, :], func=AF.Copy),
                         reads=[L_b], writes=[Sbf_b])

            if kind == "p":
                S.op("act", lambda e: e.activation(out=Sbf[:, :, :], in_=Lst[:, :, :], func=AF.Copy), reads=[L_b], writes=[Sbf_b])
                for c in range(T // 64):
                    ret_chunk(c, c * 64, col0 + c * 64)
                if col0 + T == SEG:
                    S.dma("sp", Lp_o.rearrange("(h p) v -> p h v", p=128), Lst[:, :, :], reads=[L_b], writes=[B["Lp_o"]])
            else:
                for s_ in range(NS):
                    S.dma("sp", Lst[:, :, :], st[s_].rearrange("h p v -> p h v"), writes=[L_b])
                    S.op("act", lambda e: e.activation(out=Sbf[:, :, :], in_=Lst[:, :, :], func=AF.Copy), reads=[L_b], writes=[Sbf_b])
                    ret_chunk(s_, s_ * 64, SEG + s_ * 64)
                    S.dma("sp", Ls_o[s_ * 1024:(s_ + 1) * 1024, :].rearrange("(h p) v -> p h v", p=128), Lst[:, :, :], reads=[L_b],
                          appends=[B["Ls_o"]])

        def post_exchange():
            mbs = [Bump(arena, X0, X1), Bump(arena, 0, W_EL)]

            class _MB:
                @staticmethod
                def take(n, dt=BF16):
                    ne = (n * (2 if dt == F32 else 1) + 15) // 16 * 16
                    for mb_ in mbs:
                        if mb_.off + ne <= mb_.end:
                            return mb_.take(n, dt)
                    raise AssertionError("post_exchange arena overflow")
            mb = _MB
            Li = [v3(mb.take(8 * 256, F32), 8, 256) for _ in range(2)]; Li_b = [Buf("Li0"), Buf("Li1")]
            cand = v3(mb.take(2 * 128, F32), 2, 128)
            cand_ = [v3(mb.take(2 * 128, F32), 2, 128) for _ in range(2)]; cand_b = [Buf("cd0"), Buf("cd1")]
            hK = v3(mb.take(2 * 128, F32), 2, 128); hK_b = Buf("hK")
            hV = v3(mb.take(2 * 128, F32), 2, 128); hV_b = Buf("hV")
            hVb = v3(mb.take(2 * 128), 2, 128); hVb_b = Buf("hVb")
            tt = v3(mb.take(2 * 128), 2, 128); tt_b = Buf("tt")
            S.op("dve", lambda e: e.memset(Lst[:], 0.0), writes=[L_b])
            S.op("dve", lambda e: e.memset(hK[:, :, :], 0.0), writes=[hK_b])
            S.op("dve", lambda e: e.memset(hV[:, :, :], 0.0), writes=[hV_b])
            for i in range(NC):
                k = i % 2
                base = i * XR
                S.dma("sp", Li[k][:, :, :], XD[base:base + 1024, :].rearrange("(h p) c -> p h c", p=128), reads=[B["XD"]],
                      writes=[Li_b[k]])
                for h in range(RH):
                    S.op("dve", lambda e: e.scalar_tensor_tensor(out=Lst[:, h, :], in0=Li[k][:, h, :],
                                                                 scalar=cvec[:, COEF + i * 8 + h:COEF + i * 8 + h + 1],
                                                                 in1=Lst[:, h, :], op0=ALU.mult, op1=ALU.add),
                         reads=[Li_b[k], c_b, L_b], writes=[L_b])
                for which, (acc, acc_b) in enumerate(((hK, hK_b), (hV, hV_b))):
                    r0 = base + 1024 + which * 128
                    src = XD[r0:r0 + 128, :].rearrange("r (a c) -> (r a) c", a=2).rearrange("(b p) c -> p b c", p=128)
                    kk_ = (2 * i + which) % 2
                    S.dma("sp", cand_[kk_][:, :, :], src, reads=[B["XD"]], writes=[cand_b[kk_]])
                    S.op("dve", lambda e: e.scalar_tensor_tensor(out=acc[:, :, :], in0=cand_[kk_][:, :, :],
                                                                 scalar=cvec[:, SEL + i:SEL + i + 1], in1=acc[:, :, :],
                                                                 op0=ALU.mult, op1=ALU.add),
                         reads=[cand_b[kk_], c_b, acc_b], writes=[acc_b])
            S.dma("pool", KAX[:, 0:128].rearrange("(b p) c -> p b c", p=128), hK[:, :, :], reads=[hK_b], appends=[B["KAX"]])
            S.op("act", lambda e: e.activation(out=hVb[:, :, :], in_=hV[:, :, :], func=AF.Copy), reads=[hV_b], writes=[hVb_b])
            tbv = bk[4].bitcast(BF16)

            def f(e):
                for b_ in range(2):
                    ins = e.transpose(out=tbv[:, b_ * 128:(b_ + 1) * 128], in_=hVb[:, b_, :], identity=identb[:])
                return ins
            S.op("pe", f, reads=[hVb_b, c_b], writes=[bk_b[4]])
            S.op("act", lambda e: e.activation(out=tt[:, :, :], in_=v3(tbv[:, 0:256], 2, 128), func=AF.Copy), reads=[bk_b[4]],
                 writes=[tt_b])
            S.dma("sp", VAX[0:128, :].rearrange("p (b c) -> p b c", b=2), tt[:, :, :], reads=[tt_b], appends=[B["VAX"]])

        def wout_stage(T):
            for dt in range(KT):
                slot = load_w(wout, dt * 128)
                bi = dt % 2
                mm_group(bk[bi][:, :T], bk_b[bi], slot, T)
                S.op("dve", lambda e: e.tensor_tensor(out=hT[:, dt, :T], in0=bk[bi][:, :T], in1=hT[:, dt, :T], op=ALU.add),
                     reads=[bk_b[bi], h_b], writes=[h_b])

        def ple_stage(col0, T):
            rmsnorm(T, GP)
            pTb = v3(aT[0][:, 0:2, :].rearrange("p a t -> p (a t)") if False else arena[:, X0:X0 + 2 * TM], 2, TM)
            pT_b = a_b[0]
            S.dma("pool", pTb[:, :, :T], pT[:, col0:col0 + T].rearrange("(k p) t -> p k t", p=128), writes=[pT_b])
            for dt in range(KT):
                slot = load_w(wpg, dt * 128)
                bi = dt % 2
                mm_group(bk[bi][:, :T], bk_b[bi], slot, T)
                s2 = load_w(wpp, dt * 128, rows=2)
                mm_group(bk[2 + bi][:, :T], bk_b[2 + bi], s2, T, nk=2, rhs3=pTb, rhs_b=pT_b)
                S.op("act", lambda e: e.activation(out=stt_[bi][:, :T], in_=bk[bi][:, :T], func=AF.Sigmoid), reads=[bk_b[bi]],
                     writes=[st_b[bi]])
                S.op("dve", lambda e: e.tensor_tensor(out=stt_[bi][:, :T], in0=stt_[bi][:, :T], in1=bk[2 + bi][:, :T], op=ALU.mult),
                     reads=[st_b[bi], bk_b[2 + bi]], writes=[st_b[bi]])
                S.op("dve", lambda e: e.tensor_tensor(out=hT[:, dt, :T], in0=stt_[bi][:, :T], in1=hT[:, dt, :T], op=ALU.add),
                     reads=[st_b[bi], h_b], writes=[h_b])

        def stage(n):
            if cfg.STOP <= n:
                raise _Stop()

        def program():
            tiles = [("p", i * TP, TP) for i in range(SEG // TP)] + [("s", SEG, NS * 64)]
            fm = "(kt p) t -> p kt t"
            for s_ in range(NS):
                S.dma("sp", ksT_o[s_, :, 0:64], ckT[s_, :, 64:128], appends=[B["ksT_o"]])
                S.dma("sp", vsT_o[s_, :, 0:64], cvT[s_, :, 64:128], appends=[B["vsT_o"]])
            stage(0)
            for ti, (kind, col0, T) in enumerate(tiles):
                S.dma("sp", hT[:, :, :T], xT[:, col0:col0 + T].rearrange(fm, p=128), writes=[h_b])
                ffn(T, w1g, w1u, w1d, G1)
                S.barrier()
                stage(1)
                S.dma("sp", H1[:, col0:col0 + T].rearrange(fm, p=128), hT[:, :, :T], reads=[h_b], appends=[B["H1"]])
                rmsnorm(T, GM)
                phase_a4(kind, col0, T)
                S.barrier()
                stage(2)
            stage(3)
            S.dma("sp", XS[0:1024, :].rearrange("(h p) c -> p h c", p=128), Lst[:, :, :], reads=[L_b], appends=[B["XS"]])
            S.dma("sp", XS[1024:1152, :].rearrange("r (a c) -> (r a) c", a=2), kaT_o[:, SEG - 128:SEG], reads=[B["kaT_o"]],
                  appends=[B["XS"]])
            S.dma("sp", XS[1152:1280, :].rearrange("r (a c) -> (r a) c", a=2), vaT_o[:, SEG - 128:SEG], reads=[B["vaT_o"]],
                  appends=[B["XS"]])
            S.op("pool", lambda e: e.collective_compute("AllGather", ALU.bypass, replica_groups=[list(range(NC))],
                                                        ins=[XS[:, :]], outs=[XD[:, :]]),
                 reads=[B["XS"]], writes=[B["XD"]])
            stage(4)
            post_exchange()
            S.barrier()
            stage(5)
            for ti, (kind, col0, T) in enumerate(tiles):
                S.dma("sp", hT[:, :, :T], H1[:, col0:col0 + T].rearrange(fm, p=128), reads=[B["H1"]], writes=[h_b])
                mixer(kind, col0, T)
                S.barrier()
                stage(6 if kind == "p" else 8)
                wout_stage(T)
                ffn(T, w2g, w2u, w2d, G2)
                ple_stage(col0, T)
                S.dma("sp", yT[:, col0:col0 + T].rearrange(fm, p=128), hT[:, :, :T], reads=[h_b], appends=[B["yT"]])
                S.barrier()
                stage(7 if kind == "p" else 9)

        try:
            program()
        except _Stop:
            pass
        S.finish()
        build.stats = (S.n_ops, S.n_waits, dict(S.ccnt))
    return nc


def host_constants(cfg, core):
    SEG, NS, NT = cfg.SEG, cfg.NS, cfg.NT
    seg = core % cfg.CPS
    f32 = np.float32
    lg = np.log(f32(1.0) - f32(2.0) ** (-f32(5.0) - np.arange(RH, dtype=f32))).astype(f32)
    cmat = np.zeros((128, NM), f32)
    cmat[:, M_ID:M_ID + 128] = np.eye(128, dtype=f32)
    blk = np.zeros((128, 128), f32)
    blk[:64, :64] = 1.0 / 64
    blk[64:, 64:] = 1.0 / 64
    cmat[:, M_BLK:M_BLK + 128] = blk
    perm = np.zeros((128, 128), f32)
    for m in range(128):
        perm[(m + 64) % 128, m] = 1.0
    cmat[:, M_PERM:M_PERM + 128] = perm
    idx = np.arange(64, dtype=f32)
    dec = np.exp(lg[:, None, None] * np.abs(idx[:, None] - idx[None, :])[None]).astype(f32) * f32(DK ** -0.5)
    dec = np.transpose(dec, (1, 0, 2)).reshape(64, RH * 64)
    cmat[:64, M_DEC:M_DEC + 512] = dec
    cmat[64:, M_DEC:M_DEC + 512] = dec
    qd = np.exp(lg[:, None] * (idx[None, :] + 1.0)).astype(f32)
    cmat[:, M_QDEC:M_QDEC + 512] = qd.reshape(1, RH * 64)
    mB = np.zeros((128, 256), f32)
    mB[:64, 192:] = NEG
    mB[64:, :64] = NEG
    mA = mB.copy()
    if seg == 0:
        mA[:, :128] = NEG
    cmat[:, M_MA:M_MA + 256] = mA
    cmat[:, M_MB:M_MB + 256] = mB
    half = DK // 2
    inv = (f32(ROPE_BASE) ** (-np.arange(half, dtype=f32) / f32(half))).astype(f32)
    pos = np.concatenate([np.arange(seg * SEG, (seg + 1) * SEG, dtype=f32)] + [4096.0 + np.arange(64, dtype=f32)] * NS).astype(f32)
    ang = (pos[None, :] * inv[:, None]).astype(f32)
    c = np.cos(ang).astype(f32)
    s = np.sin(ang).astype(f32)
    ropeC = np.concatenate([c, c], axis=0)
    ropeS = np.concatenate([-s, s], axis=0)
    kd = np.exp(lg[None, :] * (63.0 - idx)[:, None]).astype(f32) * f32(DK ** -0.5)
    kdec = np.concatenate([kd, kd], axis=0)
    coef = np.zeros((cfg.NCORES, RH), f32)
    sel = np.zeros((cfg.NCORES,), f32)
    for i in range(cfg.NCORES):
        if i // cfg.CPS == core // cfg.CPS and i < core:
            coef[i] = np.exp(lg * f32(SEG * (core - 1 - i))).astype(f32)
        if i == core - 1 and seg > 0:
            sel[i] = 1.0
    return cmat, ropeC.astype(f32), ropeS.astype(f32), kdec, coef.reshape(-1), sel


def fm_vec(g):
    return np.ascontiguousarray(np.asarray(g, np.float32).reshape(KT, 128).T)


def run(cfg, inputs, trace=False):
    SEG, NS, NT = cfg.SEG, cfg.NS, cfg.NT
    I = {k: np.asarray(v) for k, v in inputs.items()}
    nc = build(cfg)
    shared = {
        "w1g": I["w_ffn1_gate"][0], "w1u": I["w_ffn1_up"][0], "w1d": I["w_ffn1_down"][0],
        "w2g": I["w_ffn2_gate"][0], "w2u": I["w_ffn2_up"][0], "w2d": I["w_ffn2_down"][0],
        "win": I["w_in"][0], "wout": I["w_out"][0], "wpg": I["w_ple_gate"][0], "wpp": I["w_ple_proj"][0],
    }
    in_maps = []
    for c in range(cfg.NCORES):
        b, s = c // cfg.CPS, c % cfg.CPS
        sq = slice(c * NS, (c + 1) * NS)
        cmat, ropeC, ropeS, kdec, coef, sel = host_constants(cfg, c)
        cvec = np.zeros((128, NV), np.float32)
        cvec[:, G1:G1 + 32] = fm_vec(I["g_ffn1"][0]); cvec[:, GM:GM + 32] = fm_vec(I["g_mix"][0])
        cvec[:, G2:G2 + 32] = fm_vec(I["g_ffn2"][0]); cvec[:, GP:GP + 32] = fm_vec(I["g_ple"][0])
        cvec[:, GQ] = np.tile(I["g_q"][0], 2); cvec[:, GK] = np.tile(I["g_k"][0], 2)
        sink_perm = [g_ * 8 + 2 * (s_ % 4) + s_ // 4 for g_ in range(NKV) for s_ in range(8)]
        cvec[:, SINK:SINK + 32] = I["attn_sinks"][0][sink_perm][None, :]
        cvec[:, GRET:GRET + 16] = I["g_ret"][0].reshape(16, 128).T
        cvec[:, KDEC:KDEC + 8] = kdec
        cvec[:, COEF:COEF + coef.size] = coef[None, :]
        cvec[:, SEL:SEL + sel.size] = sel[None, :]
        xs = I["x_sample"][sq].reshape(NS * 64, D)
        ps_ = I["p_sample"][0, sq].reshape(NS * 64, PLE)
        m = dict(shared)
        m.update({
            "xT": np.ascontiguousarray(np.concatenate([I["x_prompt"][b, s * SEG:(s + 1) * SEG], xs], axis=0).T),
            "pT": np.ascontiguousarray(np.concatenate([I["p_prompt"][0, b, s * SEG:(s + 1) * SEG], ps_], axis=0).T),
            "ckT": np.ascontiguousarray(I["cache_attn_k"][0, sq].reshape(NS, 128, 256).transpose(0, 2, 1)),
            "cvT": np.ascontiguousarray(I["cache_attn_v"][0, sq].reshape(NS, 128, 256).transpose(0, 2, 1)),
            "cv": np.ascontiguousarray(I["cache_attn_v"][0, sq].reshape(NS, 128, 256)),
            "st": np.ascontiguousarray(I["state_ret"][0, sq]),
            "cvec": cvec, "cmat": cmat, "ropeC": ropeC, "ropeS": ropeS,
        })
        in_maps.append(m)
    res = run_bass_kernel_spmd(nc, in_maps, core_ids=list(range(cfg.NCORES)), **({"trace": True} if trace else {}))
    if cfg.STOP < 99:
        return None, res
    R = res.results
    NB = cfg.NCORES // cfg.CPS
    SEQ = SEG * cfg.CPS
    DB = NS * cfg.NCORES
    y_p = np.zeros((NB, SEQ, D), np.float32); y_s = np.zeros((DB, 64, D), np.float32)
    kp = np.zeros((1, NB, 128, NKV, HD), np.float32); vp = np.zeros_like(kp)
    sp_ = np.zeros((1, NB, RH, DK, DV), np.float32)
    ks = np.zeros((1, DB, 128, NKV, HD), np.float32); vs = np.zeros_like(ks)
    ss = np.zeros((1, DB, RH, DK, DV), np.float32)
    for c in range(cfg.NCORES):
        b, s = c // cfg.CPS, c % cfg.CPS
        r = R[c]
        yt = r["yT"].T
        y_p[b, s * SEG:(s + 1) * SEG] = yt[:SEG]
        y_s[c * NS:(c + 1) * NS] = yt[SEG:].reshape(NS, 64, D)
        if s == cfg.CPS - 1:
            kp[0, b] = r["kaT_o"][:, SEG - 128:SEG].T.reshape(128, NKV, HD)
            vp[0, b] = r["vaT_o"][:, SEG - 128:SEG].T.reshape(128, NKV, HD)
            sp_[0, b] = r["Lp_o"].reshape(RH, DK, DV)
        ks[0, c * NS:(c + 1) * NS] = r["ksT_o"].transpose(0, 2, 1).reshape(NS, 128, NKV, HD)
        vs[0, c * NS:(c + 1) * NS] = r["vsT_o"].transpose(0, 2, 1).reshape(NS, 128, NKV, HD)
        ss[0, c * NS:(c + 1) * NS] = r["Ls_o"].reshape(NS, RH, DK, DV)
    return (y_p, y_s, kp, vp, sp_, ks, vs, ss), res


def kernel(**inputs):
    cfg = Cfg()
    outs, _ = run(cfg, inputs)
    return outs
```
